# Optimizing a Trainium2 kernel written in Bass

```python
import jax, jax.numpy as jnp
from jax import lax
import numpy as np

D_MODEL = 2048
BATCH = 1
SEQ = 16384
DEPTH = 1

N_HEADS = 8
N_KV_HEADS = 2
HEAD_DIM = 128
ATTN_WIDTH = N_HEADS * HEAD_DIM
KV_WIDTH = N_KV_HEADS * HEAD_DIM
Q_BLOCK = 128
ROPE_THETA = 10000.0
ROPE_PAIRS = HEAD_DIM // 4
GRID_W = 64
LRU_WIDTH = D_MODEL // 2
LRU_BLOCKS = 8
LRU_BW = LRU_WIDTH // LRU_BLOCKS
LRU_C = 8.0
CONV_W = 4
CONV_PAD = (2, 1)
MIX_WIDTH = ATTN_WIDTH + LRU_WIDTH
IN_WIDTH = ATTN_WIDTH + 2 * KV_WIDTH + ATTN_WIDTH + 2 * LRU_WIDTH
SPLITS = (ATTN_WIDTH,
          ATTN_WIDTH + KV_WIDTH,
          ATTN_WIDTH + 2 * KV_WIDTH,
          2 * ATTN_WIDTH + 2 * KV_WIDTH,
          2 * ATTN_WIDTH + 2 * KV_WIDTH + LRU_WIDTH)
EPS = 1e-6

kernel_name = "hymba_griffin_axialrope_hybrid_block"


def rms_norm(x, w):
    xf = x.astype(jnp.float32)
    y = xf * lax.rsqrt(jnp.mean(xf * xf, axis=-1, keepdims=True) + EPS)
    return (y * w.astype(jnp.float32)).astype(x.dtype)


def axial_rope_tables(seq_len):
    rows = seq_len // GRID_W
    row = jnp.repeat(jnp.arange(rows, dtype=jnp.float32), GRID_W)
    col = jnp.tile(jnp.arange(GRID_W, dtype=jnp.float32), rows)
    inv_freq = ROPE_THETA ** (-jnp.arange(ROPE_PAIRS, dtype=jnp.float32) / ROPE_PAIRS)
    ang_r = row[:, None] * inv_freq[None, :]
    ang_c = col[:, None] * inv_freq[None, :]
    return jnp.cos(ang_r), jnp.sin(ang_r), jnp.cos(ang_c), jnp.sin(ang_c)


def rope_half(x, cos, sin):
    c = cos[None, :, None, :]
    s = sin[None, :, None, :]
    x1, x2 = jnp.split(x, 2, axis=-1)
    return jnp.concatenate([x1 * c - x2 * s, x2 * c + x1 * s], axis=-1)


def apply_axial_rope(x, tables):
    cr, sr, cc, sc = tables
    xf = x.astype(jnp.float32)
    x_row, x_col = jnp.split(xf, 2, axis=-1)
    out = jnp.concatenate([rope_half(x_row, cr, sr), rope_half(x_col, cc, sc)], axis=-1)
    return out.astype(x.dtype)


def attention_group(q, k, v, q_norm_w, k_norm_w):
    B, S, _ = q.shape
    G = N_HEADS // N_KV_HEADS
    tables = axial_rope_tables(S)
    q = rms_norm(q.reshape(B, S, N_HEADS, HEAD_DIM), q_norm_w)
    k = rms_norm(k.reshape(B, S, N_KV_HEADS, HEAD_DIM), k_norm_w)
    q = apply_axial_rope(q, tables)
    k = apply_axial_rope(k, tables)
    v = v.reshape(B, S, N_KV_HEADS, HEAD_DIM)
    scale = HEAD_DIM ** -0.5
    nb = S // Q_BLOCK
    q_blocks = q.reshape(B, nb, Q_BLOCK, N_KV_HEADS, G, HEAD_DIM).transpose(1, 0, 3, 4, 2, 5)
    k_t = k.transpose(0, 2, 1, 3)
    v_t = v.transpose(0, 2, 1, 3)

    def block(qb):
        s = jnp.einsum('bkgqd,bksd->bkgqs', qb.astype(jnp.float32), k_t.astype(jnp.float32)) * scale
        p = jax.nn.softmax(s, axis=-1)
        return jnp.einsum('bkgqs,bksd->bkgqd', p, v_t.astype(jnp.float32)).astype(q.dtype)

    out = lax.map(block, q_blocks)
    return out.transpose(1, 0, 4, 2, 3, 5).reshape(B, S, ATTN_WIDTH)


def block_diag(x, w):
    B, S, _ = x.shape
    xb = x.reshape(B, S, LRU_BLOCKS, LRU_BW)
    return jnp.einsum('bsnc,ncd->bsnd', xb, w).reshape(B, S, LRU_WIDTH)


def linear_scan(a, b, reverse):
    def op(c1, c2):
        a1, b1 = c1
        a2, b2 = c2
        return a1 * a2, a2 * b1 + b2
    _, h = lax.associative_scan(op, (a, b), axis=1, reverse=reverse)
    return h


def rg_lru_direction(xf, wa, ba, wx, bx, lam, reverse):
    r = jax.nn.sigmoid(block_diag(xf, wa) + ba)
    i = jax.nn.sigmoid(block_diag(xf, wx) + bx)
    log_a = -LRU_C * r * jax.nn.softplus(-lam)
    a = jnp.exp(log_a)
    mult = jnp.sqrt(-jnp.expm1(2.0 * log_a))
    return linear_scan(a, mult * (i * xf), reverse)


def lru_group(xr, conv_w, conv_b, lru_wa, lru_ba, lru_wx, lru_bx, lru_lambda):
    xc = lax.conv_general_dilated(xr, conv_w.astype(xr.dtype), window_strides=(1,),
                                  padding=[CONV_PAD],
                                  dimension_numbers=('NWC', 'WIO', 'NWC'),
                                  feature_group_count=LRU_WIDTH) + conv_b
    xf = xc.astype(jnp.float32)
    h_fwd = rg_lru_direction(xf, lru_wa[0].astype(jnp.float32), lru_ba[0].astype(jnp.float32),
                             lru_wx[0].astype(jnp.float32), lru_bx[0].astype(jnp.float32),
                             lru_lambda[0].astype(jnp.float32), reverse=False)
    h_bwd = rg_lru_direction(xf, lru_wa[1].astype(jnp.float32), lru_ba[1].astype(jnp.float32),
                             lru_wx[1].astype(jnp.float32), lru_bx[1].astype(jnp.float32),
                             lru_lambda[1].astype(jnp.float32), reverse=True)
    return (h_fwd + h_bwd).astype(xr.dtype)


def setup_inputs(seed: int = 0) -> dict:
    key = jax.random.key(seed)
    ks = jax.random.split(key, 16)
    f32 = jnp.float32
    x = jax.random.normal(ks[0], (BATCH, SEQ, D_MODEL), f32)
    norm_w = 1.0 + 0.02 * jax.random.normal(ks[1], (D_MODEL,), f32)
    w_in = jax.random.normal(ks[2], (D_MODEL, IN_WIDTH), f32) * D_MODEL ** -0.5
    q_norm_w = 1.0 + 0.02 * jax.random.normal(ks[3], (HEAD_DIM,), f32)
    k_norm_w = 1.0 + 0.02 * jax.random.normal(ks[4], (HEAD_DIM,), f32)
    conv_w = jax.random.normal(ks[5], (CONV_W, 1, LRU_WIDTH), f32) * CONV_W ** -0.5
    conv_b = 0.01 * jax.random.normal(ks[6], (LRU_WIDTH,), f32)
    lru_wa = jax.random.normal(ks[7], (2, LRU_BLOCKS, LRU_BW, LRU_BW), f32) * LRU_BW ** -0.5
    lru_ba = 0.01 * jax.random.normal(ks[8], (2, LRU_WIDTH), f32)
    lru_wx = jax.random.normal(ks[9], (2, LRU_BLOCKS, LRU_BW, LRU_BW), f32) * LRU_BW ** -0.5
    lru_bx = 0.01 * jax.random.normal(ks[10], (2, LRU_WIDTH), f32)
    u = jax.random.uniform(ks[11], (2, LRU_WIDTH), f32, minval=0.9, maxval=0.999)
    a0 = u ** (1.0 / LRU_C)
    lru_lambda = jnp.log(a0) - jnp.log1p(-a0)
    attn_norm_w = 1.0 + 0.02 * jax.random.normal(ks[12], (ATTN_WIDTH,), f32)
    lru_norm_w = 1.0 + 0.02 * jax.random.normal(ks[13], (LRU_WIDTH,), f32)
    w_out = jax.random.normal(ks[14], (MIX_WIDTH, D_MODEL), f32) * MIX_WIDTH ** -0.5
    return {"x": x, "norm_w": norm_w, "w_in": w_in, "q_norm_w": q_norm_w, "k_norm_w": k_norm_w,
            "conv_w": conv_w, "conv_b": conv_b, "lru_wa": lru_wa, "lru_ba": lru_ba,
            "lru_wx": lru_wx, "lru_bx": lru_bx, "lru_lambda": lru_lambda,
            "attn_norm_w": attn_norm_w, "lru_norm_w": lru_norm_w, "w_out": w_out}


def hybrid_layer(x, norm_w, w_in, q_norm_w, k_norm_w, conv_w, conv_b, lru_wa, lru_ba, lru_wx,
                 lru_bx, lru_lambda, attn_norm_w, lru_norm_w, w_out):
    h = rms_norm(x, norm_w)
    proj = jnp.einsum('bsd,de->bse', h, w_in)
    q, k, v, g_attn, xr, g_lru = jnp.split(proj, SPLITS, axis=-1)
    attn_out = attention_group(q, k, v, q_norm_w, k_norm_w)
    lru_out = lru_group(xr, conv_w, conv_b, lru_wa, lru_ba, lru_wx, lru_bx, lru_lambda)
    mixed = jnp.concatenate([rms_norm(attn_out, attn_norm_w) * jax.nn.silu(g_attn),
                             rms_norm(lru_out, lru_norm_w) * jax.nn.silu(g_lru)], axis=-1)
    return x + jnp.einsum('bse,ed->bsd', mixed, w_out)


def reference(x, norm_w, w_in, q_norm_w, k_norm_w, conv_w, conv_b, lru_wa, lru_ba, lru_wx,
              lru_bx, lru_lambda, attn_norm_w, lru_norm_w, w_out):
    for _ in range(DEPTH):
        x = hybrid_layer(x, norm_w, w_in, q_norm_w, k_norm_w, conv_w, conv_b, lru_wa, lru_ba,
                         lru_wx, lru_bx, lru_lambda, attn_norm_w, lru_norm_w, w_out)
    return x
```

```python
import contextlib
import os

import ml_dtypes
import numpy as np

import concourse.bass as bass
import concourse.mybir as mybir
from concourse.bass_utils import run_bass_kernel_spmd

F32, BF16 = mybir.dt.float32, mybir.dt.bfloat16
ALU = mybir.AluOpType
AF = mybir.ActivationFunctionType
AX = mybir.AxisListType

NCORES = 8
S = 16384
T = 2048
NT = 16
TE = T + 128
D = 2048
KC = 16
EPS = 1e-6
SCALE = 128 ** -0.5
DEBUG = bool(int(os.environ.get("MK_DEBUG", "0")))
PHASES = os.environ.get("MK_PHASES", "A,B,L1,L2,C,D").split(",")


class Prog:
    ENGS = ("sync", "scalar", "vector", "gpsimd", "tensor")

    def __init__(self, nc, sems, dma_sems):
        self.nc = nc
        self.sems = sems
        self.dma_sems = dma_sems
        self.ops = {e: [] for e in self.ENGS}
        self.cnt = {e: 0 for e in self.ENGS}
        self.dma_i = {e: 0 for e in self.ENGS}
        self.waited = {e: {} for e in self.ENGS}
        self.dma_last = {}

    def _waits(self, eng, deps):
        w = []
        for d in deps:
            if d is None:
                continue
            sem, val = d
            key = id(sem)
            if self.waited[eng].get(key, 0) >= val:
                continue
            self.waited[eng][key] = val
            w.append((sem, val))
        return w

    def op(self, eng, fn, deps=(), sig=True):
        w = self._waits(eng, deps)
        ev = None
        if sig:
            self.cnt[eng] += 1
            ev = (self.sems[eng], self.cnt[eng])
        self.ops[eng].append((w, fn, (self.sems[eng], 1) if sig else None))
        return ev

    def dma(self, eng, out, in_, deps=(), **kw):
        pool = self.dma_sems[eng]
        i = self.dma_i[eng]
        self.dma_i[eng] += 1
        sem = pool[i % len(pool)]
        rnd = i // len(pool)
        deps = list(deps)
        if rnd > 0:
            deps.append((sem, 16 * rnd))
        w = self._waits(eng, deps)
        self.ops[eng].append((w, lambda e: e.dma_start(out=out, in_=in_, **kw), (sem, 16)))
        ev = (sem, 16 * (rnd + 1))
        self.dma_last[id(sem)] = ev
        return ev

    def finish(self):
        self.op("sync", lambda e: e.nop(), deps=list(self.dma_last.values()), sig=False)

    def run(self, block):
        for eng in self.ENGS:
            ops = self.ops[eng]
            if not ops:
                continue

            def body(e, ops=ops):
                for w, fn, inc in ops:
                    for sem, val in w:
                        e.wait_ge(sem, val)
                    ins = fn(e)
                    if inc is not None:
                        ins.then_inc(inc[0], inc[1])
            getattr(block, eng)(body)


_SHARED = {}


@contextlib.contextmanager
def phase(nc, name, nds=8):
    st = _SHARED[id(nc)]
    with contextlib.ExitStack() as es:
        P = Prog(nc, st["sems"], st["dsems"])
        P.cnt, P.dma_i, P.waited, P.dma_last = st["cnt"], st["dma_i"], st["waited"], {}
        snap = (dict(P.cnt), dict(P.dma_i), {e: dict(v) for e, v in P.waited.items()})
        P.sb = lambda n, s, d: es.enter_context(nc.sbuf_tensor(f"{name}_{n}", s, d))
        P.ps = lambda n, s, d: es.enter_context(nc.psum_tensor(f"{name}_{n}", s, d))
        yield P
        if name in PHASES:
            P.finish()
            with nc.Block() as block:
                P.run(block)
        else:
            st["cnt"].update(snap[0])
            st["dma_i"].update(snap[1])
            for e in Prog.ENGS:
                st["waited"][e] = snap[2][e]


def init_shared(nc, top, nds=8):
    sems = {e: top.enter_context(nc.semaphore(f"s_{e}")) for e in Prog.ENGS}
    dsems = {e: [top.enter_context(nc.semaphore(f"d_{e}{i}")) for i in range(nds)] for e in ("sync", "gpsimd")}
    _SHARED[id(nc)] = dict(sems=sems, dsems=dsems, cnt={e: 0 for e in Prog.ENGS}, dma_i={e: 0 for e in Prog.ENGS},
                           waited={e: {} for e in Prog.ENGS})


def act(out, in_, func, **kw):
    return lambda e: e.activation(out=out, in_=in_, func=func, **kw)


def tt(out, in0, in1, op):
    return lambda e: e.tensor_tensor(out=out, in0=in0, in1=in1, op=op)


def ts(out, in0, s1, s2, op0, op1=None):
    if op1 is None:
        return lambda e: e.tensor_scalar(out=out, in0=in0, scalar1=s1, scalar2=None, op0=op0)
    return lambda e: e.tensor_scalar(out=out, in0=in0, scalar1=s1, scalar2=s2, op0=op0, op1=op1)


def stt(out, in0, scalar, in1, op0, op1):
    return lambda e: e.scalar_tensor_tensor(out=out, in0=in0, scalar=scalar, in1=in1, op0=op0, op1=op1)


def cp(out, in_):
    return lambda e: e.tensor_copy(out=out, in_=in_)


def mm(out, lhsT, rhs, start, stop, skip=False):
    if skip:
        return lambda e: e.matmul(out, lhsT, rhs, start=start, stop=stop, skip_group_check=True)
    return lambda e: e.matmul(out, lhsT, rhs, start=start, stop=stop)


def tr(out, in_, ident):
    return lambda e: e.transpose(out, in_, ident)


def rstd_chain(P, ss, ms, sd, rs, inv_n, dep):
    e1 = P.op("vector", ts(ms, ss, inv_n, EPS, ALU.mult, ALU.add), deps=[dep])
    e2 = P.op("scalar", act(sd, ms, AF.Sqrt), deps=[e1])
    return P.op("vector", lambda e: e.reciprocal(out=rs, in_=sd), deps=[e2])


def build_program():
    nc = bass.Bass("TRN2", target_bir_lowering=False)
    I = {}

    def inp(name, shape, dt=F32):
        I[name] = nc.dram_tensor(name, list(shape), dt, kind="ExternalInput").ap()
        return I[name]

    xe = inp("xe", [TE, D])
    w_in = inp("w_in", [D, 4608])
    w_out = inp("w_out", [D, D])
    norm_w = inp("norm_w", [D])
    qnw = inp("q_norm_w", [128])
    knw = inp("k_norm_w", [128])
    anw = inp("attn_norm_w", [1024])
    ropeC = inp("ropeC", [T, 128])
    ropeS = inp("ropeS", [T, 128])
    ident_d = inp("ident", [128, 128], BF16)
    rank1h = inp("rank1h", [128, 8])
    convw = inp("convw", [128, 32])
    convb = inp("convb", [128, 8])
    lba = inp("lba", [128, 16])
    lbx = inp("lbx", [128, 16])
    llam = inp("llam", [128, 16])
    lnw = inp("lnw", [128, 8])
    lwa = inp("lru_wa", [2, 8, 128, 128])
    lwx = inp("lru_wx", [2, 8, 128, 128])
    out = nc.dram_tensor("out", [T, D], F32, kind="ExternalOutput").ap()

    dbg_kind = dict(kind="ExternalOutput") if DEBUG else {}

    def scratch(name, shape, dt):
        return nc.dram_tensor(name, list(shape), dt, **dbg_kind)

    kT_loc = nc.dram_tensor("kT_loc", [128, 2 * T], BF16)
    kT_all = nc.dram_tensor("kT_all", [NCORES * 128, 2 * T], BF16)
    v_loc = nc.dram_tensor("v_loc", [128, NT * 258], BF16)
    v_all = nc.dram_tensor("v_all", [NCORES * 128, NT * 258], BF16)
    cr_loc = nc.dram_tensor("cr_loc", [128, 32], F32)
    cr_all = nc.dram_tensor("cr_all", [NCORES * 128, 32], F32)
    qT_d = scratch("qT_d", [128, 8 * T], BF16).ap()
    sga_d = scratch("sga_d", [T, 1024], F32).ap()
    sgl_d = scratch("sgl_d", [1024, T], F32).ap()
    xrT_d = scratch("xrT_d", [1024, T + 3], F32).ap()
    ab_d = scratch("ab_d", [32, 128, T], F32).ap()
    mixL_d = scratch("mixL_d", [128, 8 * T], BF16).ap()
    ao_d = scratch("ao_d", [T, 1024], F32).ap()

    with contextlib.ExitStack() as top:
        init_shared(nc, top)
        cc_kv = top.enter_context(nc.semaphore("cc_kv"))
        cc_cr = top.enter_context(nc.semaphore("cc_cr"))
        ident = top.enter_context(nc.sbuf_tensor("ident_sb", [128, 128], BF16))
        st_ = _SHARED[id(nc)]
        all_sems = list(st_["sems"].values()) + [x for v in st_["dsems"].values() for x in v] + [cc_kv, cc_cr]

        def clear_all():
            with nc.Block() as blk:
                def body(e):
                    for sm in all_sems:
                        e.sem_clear(sm)
                blk.gpsimd(body)

        clear_all()

        with contextlib.ExitStack() as esAB:
            hT = esAB.enter_context(nc.sbuf_tensor("hT", [128, KC, TE], BF16))
            WB = [esAB.enter_context(nc.sbuf_tensor(f"WB{i}", [128, KC, 512], BF16)) for i in range(2)]
            stg = [esAB.enter_context(nc.sbuf_tensor(f"stg{i}", [128, 4, 512], F32)) for i in range(2)]
            blocks = [1024, 0, 512, 1536, 2048, 2560, 3072, 3584, 4096]
            wb_ev = {}
            wb_free = {}
            stg_free = {0: None, 1: None}
            qctr = [0]

            def load_quarter(P, j, qd):
                i = qctr[0]
                qctr[0] += 1
                sb_ = i % 2
                src = w_in[qd * 512:(qd + 1) * 512, blocks[j]:blocks[j] + 512].rearrange("(kc p) c -> p kc c", p=128)
                e_ld = P.dma("sync", stg[sb_][:], src, deps=[stg_free[sb_]])
                e_cv = P.op("gpsimd", cp(WB[j % 2][:, qd * 4:(qd + 1) * 4, :], stg[sb_][:]),
                            deps=[e_ld, wb_free.get(j - 2)])
                stg_free[sb_] = e_cv
                wb_ev.setdefault(j, []).append(e_cv)

            with phase(nc, "A") as P:
                xs = [P.sb(f"xs{i}", [128, D], F32) for i in range(2)]
                hb = [P.sb(f"hb{i}", [128, D], BF16) for i in range(2)]
                junk = P.sb("junk", [128, D], BF16)
                nwb = P.sb("nwb", [128, D], F32)
                ss = P.sb("ss", [128, NT + 1], F32)
                ms = P.sb("ms", [128, NT + 1], F32)
                sd = P.sb("sd", [128, NT + 1], F32)
                rs = P.sb("rs", [128, NT + 1], F32)
                tp = [P.ps(f"tp{i}", [128, 1024], BF16) for i in range(2)]
                e_id = P.dma("sync", ident[:], ident_d)
                e_nw = P.dma("sync", nwb[:], norm_w.partition_broadcast(128))
                stt_ev = {}
                sq_ev = {}
                tr_ev = {}
                evac_ev = {0: None, 1: None}
                for t in range(NT + 1):
                    s = t % 2
                    ld = P.dma("sync", xs[s][:], xe[t * 128:(t + 1) * 128, :],
                               deps=[stt_ev.get(t - 2), sq_ev.get(t - 2)])
                    if t < 8:
                        load_quarter(P, t // 4, t % 4)
                    sq = P.op("scalar", act(junk[:], xs[s][:], AF.Square, accum_out=ss[:, t:t + 1]), deps=[ld])
                    sq_ev[t] = sq
                    r1 = rstd_chain(P, ss[:, t:t + 1], ms[:, t:t + 1], sd[:, t:t + 1], rs[:, t:t + 1], 1.0 / D, sq)
                    st = P.op("vector", stt(hb[s][:], xs[s][:], rs[:, t:t + 1], nwb[:], ALU.mult, ALU.mult),
                              deps=[r1, ld, e_nw, tr_ev.get(t - 2)])
                    stt_ev[t] = st
                    for g in range(2):
                        last = None
                        for i in range(8):
                            kc = g * 8 + i
                            last = P.op("tensor", tr(tp[g][:, i * 128:(i + 1) * 128], hb[s][:, kc * 128:(kc + 1) * 128],
                                                     ident[:]), deps=[st, e_id, evac_ev[g]], sig=(i == 7))
                        tr_ev[t] = last
                        dst = hT[:, g * 8:(g + 1) * 8, t * 128:(t + 1) * 128]
                        src = tp[g][:].rearrange("p (k n) -> p k n", k=8)
                        if g == 0:
                            evac_ev[g] = P.op("vector", cp(dst, src), deps=[last])
                        else:
                            evac_ev[g] = P.op("scalar", act(dst, src, AF.Copy), deps=[last])

            with phase(nc, "B") as P:
                psB = [P.ps(f"ps{i}", [128, 512], F32) for i in range(3)]
                tpB = [P.ps(f"tp{i}", [128, 512], BF16) for i in range(2)]
                KTl = P.sb("KTl", [128, 2, T], BF16)
                Vl = P.sb("Vl", [128, NT, 2, 129], BF16)
                Cf = P.sb("Cf", [128, NT, 128], F32)
                Sf = P.sb("Sf", [128, NT, 128], F32)
                wq = P.sb("wq", [128, 128], F32)
                wqs = P.sb("wqs", [128, 128], F32)
                wk = P.sb("wk", [128, 128], F32)
                wks = P.sb("wks", [128, 128], F32)
                TC = [P.sb(f"TC{i}", [128, 128], F32) for i in range(2)]
                TS = [P.sb(f"TS{i}", [128, 128], F32) for i in range(2)]
                qs = [P.sb(f"qs{i}", [128, 512], F32) for i in range(2)]
                t1 = [P.sb(f"t1{i}", [128, 512], F32) for i in range(2)]
                t2 = [P.sb(f"t2{i}", [128, 512], F32) for i in range(2)]
                qr = [P.sb(f"qr{i}", [128, 512], BF16) for i in range(3)]
                qTs = [P.sb(f"qTs{i}", [128, 4, 128], BF16) for i in range(2)]
                junk = P.sb("junk", [128, 128], F32)
                ssq = P.sb("ssq", [128, 8 * NT * 3], F32)
                msq = P.sb("msq", [128, 8 * NT * 3], F32)
                sdq = P.sb("sdq", [128, 8 * NT * 3], F32)
                rsq = P.sb("rsq", [128, 8 * NT * 3], F32)
                th = [P.sb(f"th{i}", [128, 512], F32) for i in range(2)]
                sg = [P.sb(f"sg{i}", [128, 512], F32) for i in range(2)]
                xst = [P.sb(f"xst{i}", [128, T + 3], F32) for i in range(2)]

                e_c = P.dma("sync", Cf[:], ropeC.rearrange("(t p) d -> p t d", p=128))
                e_s = P.dma("sync", Sf[:], ropeS.rearrange("(t p) d -> p t d", p=128))
                e_w = []
                for wsrc, wdst, wsw in ((qnw, wq, wqs), (knw, wk, wks)):
                    e_w.append(P.dma("sync", wdst[:], wsrc.partition_broadcast(128)))
                    for (a, b) in ((0, 32), (32, 0), (64, 96), (96, 64)):
                        e_w.append(P.dma("sync", wsw[:, a:a + 32], wsrc[b:b + 32].partition_broadcast(128)))
                e_one = P.op("gpsimd", lambda e: e.memset(Vl[:, :, :, 128:129], 1.0))

                def load_block(j):
                    for qd in range(4):
                        load_quarter(P, j, qd)

                ps_free = {}
                ps_i = [0]
                nrm_i = [0]
                tp_free = {0: None, 1: None}
                tp_i = [0]
                stage_free = {}

                def proj_tokmajor(j, t):
                    k = ps_i[0] % 3
                    ps_i[0] += 1
                    last = None
                    for kc in range(KC):
                        last = P.op("tensor", mm(psB[k][:], hT[:, kc, t * 128:(t + 1) * 128], WB[j % 2][:, kc, :],
                                                 kc == 0, kc == KC - 1),
                                    deps=[wb_ev[j][kc // 4]] + (ps_free.get(k, []) if kc == 0 else []),
                                    sig=(kc == KC - 1))
                    return k, last

                def proj_featmajor(j, c, tok0, n):
                    k = ps_i[0] % 3
                    ps_i[0] += 1
                    last = None
                    for kc in range(KC):
                        last = P.op("tensor", mm(psB[k][:, 0:n], WB[j % 2][:, kc, c * 128:(c + 1) * 128],
                                                 hT[:, kc, tok0:tok0 + n], kc == 0, kc == KC - 1),
                                    deps=[wb_ev[j][kc // 4]] + (ps_free.get(k, []) if kc == 0 else []),
                                    sig=(kc == KC - 1))
                    return k, last

                def nr_front(t, k, ev, c0, nh, wt, wst):
                    n = nh * 128
                    i = nrm_i[0]
                    nrm_i[0] += 1
                    s_ = i % 2
                    qi = i % 3
                    src = psB[k][:, c0:c0 + n]
                    col = i * 8
                    war = stage_free.get(("qs", s_), [])
                    evs_ps = []
                    for h in range(nh):
                        evs_ps.append(P.op("scalar", act(junk[:], src[:, h * 128:(h + 1) * 128], AF.Square,
                                                         accum_out=ssq[:, col + h:col + h + 1]), deps=[ev]))
                    e_cp = P.op("scalar", act(qs[s_][:, 0:n], src, AF.Copy), deps=[ev] + war)
                    evs_ps.append(e_cp)
                    e_r = rstd_chain(P, ssq[:, col:col + nh], msq[:, col:col + nh], sdq[:, col:col + nh],
                                     rsq[:, col:col + nh], 1.0 / 128, evs_ps[nh - 1])
                    e_tc = P.op("vector", tt(TC[s_][:], Cf[:, t, :], wt[:], ALU.mult),
                                deps=[e_c] + e_w + stage_free.get(("TC", s_), []))
                    e_ts = P.op("vector", tt(TS[s_][:], Sf[:, t, :], wst[:], ALU.mult),
                                deps=[e_s] + e_w + stage_free.get(("TC", s_), []))
                    q3 = qs[s_][:, 0:n].rearrange("p (h d) -> p h d", h=nh)
                    e_t1 = P.op("vector", tt(t1[s_][:, 0:n].rearrange("p (h d) -> p h d", h=nh), q3,
                                             TC[s_][:].unsqueeze(1).to_broadcast([128, nh, 128]), ALU.mult),
                                deps=[e_cp, e_tc] + stage_free.get(("t1", s_), []))
                    q5 = qs[s_][:, 0:n].rearrange("p (h a b c) -> p h a b c", h=nh, a=2, b=2)
                    o5 = t2[s_][:, 0:n].rearrange("p (h a b c) -> p h a b c", h=nh, a=2, b=2)
                    s4 = TS[s_][:].rearrange("p (a b c) -> p a b c", a=2, b=2)
                    e_t2 = []
                    for b_ in range(2):
                        e_t2.append(P.op("gpsimd", tt(o5[:, :, :, b_, :], q5[:, :, :, 1 - b_, :],
                                                      s4[:, :, b_, :].unsqueeze(1).to_broadcast([128, nh, 2, 32]),
                                                      ALU.mult),
                                         deps=[e_cp, e_ts] + stage_free.get(("t2", s_), [])))
                    e_add = P.op("vector", tt(t1[s_][:, 0:n], t1[s_][:, 0:n], t2[s_][:, 0:n], ALU.add),
                                 deps=[e_t1] + e_t2)
                    e_q = P.op("vector", tt(qr[qi][:, 0:n].rearrange("p (h d) -> p h d", h=nh),
                                            t1[s_][:, 0:n].rearrange("p (h d) -> p h d", h=nh),
                                            rsq[:, col:col + nh].unsqueeze(2).to_broadcast([128, nh, 128]), ALU.mult),
                                 deps=[e_add, e_r] + stage_free.get(("qr", qi), []))
                    stage_free[("qs", s_)] = [e_t1] + e_t2
                    stage_free[("TC", s_)] = [e_t1] + e_t2
                    stage_free[("t1", s_)] = [e_q]
                    stage_free[("t2", s_)] = [e_add]
                    return evs_ps, (qi, e_q, nh)

                def nr_back(ctx, dst_fn, dst_war=()):
                    qi, e_q, nh = ctx
                    g = tp_i[0] % 2
                    tp_i[0] += 1
                    last = None
                    for h in range(nh):
                        last = P.op("tensor", tr(tpB[g][:, h * 128:(h + 1) * 128], qr[qi][:, h * 128:(h + 1) * 128],
                                                 ident[:]), deps=[e_q, tp_free[g]], sig=(h == nh - 1))
                    stage_free[("qr", qi)] = [last]
                    outs = []
                    for h in range(nh):
                        outs.append(P.op("vector", cp(dst_fn(h), tpB[g][:, h * 128:(h + 1) * 128]),
                                         deps=[last] + list(dst_war)))
                    tp_free[g] = outs[-1]
                    return outs

                DELAY = 2

                kv_done = []
                j = 0
                pend = []
                for t in range(NT):
                    k, ev = proj_tokmajor(j, t)
                    if len(pend) >= DELAY:
                        t_, ctx = pend.pop(0)
                        kv_done += nr_back(ctx, lambda h, t_=t_: KTl[:, h, t_ * 128:(t_ + 1) * 128])
                    e_v = P.op("scalar", act(Vl[:, t, :, 0:128], psB[k][:, 256:512].rearrange("p (h d) -> p h d", h=2),
                                             AF.Copy), deps=[ev, e_one])
                    evs_ps, ctx = nr_front(t, k, ev, 0, 2, wk, wks)
                    pend.append((t, ctx))
                    ps_free[k] = evs_ps + [e_v]
                    kv_done += [e_v]
                    if t == NT - 1:
                        wb_free[j] = ev
                for t_, ctx in pend:
                    kv_done += nr_back(ctx, lambda h, t_=t_: KTl[:, h, t_ * 128:(t_ + 1) * 128])
                load_block(2)
                e_k1 = P.dma("sync", kT_loc.ap(), KTl[:].rearrange("p h t -> p (h t)"), deps=kv_done)
                e_v1 = P.dma("sync", v_loc.ap(), Vl[:].rearrange("p t h d -> p (t h d)"), deps=kv_done)
                P.op("gpsimd", lambda e: e.collective_compute(
                    "AllGather", ALU.bypass, replica_groups=[list(range(NCORES))],
                    ins=[kT_loc.ap().opt()], outs=[kT_all.ap().opt()]).then_inc(cc_kv), deps=[e_k1, e_v1], sig=False)
                P.op("gpsimd", lambda e: e.collective_compute(
                    "AllGather", ALU.bypass, replica_groups=[list(range(NCORES))],
                    ins=[v_loc.ap().opt()], outs=[v_all.ap().opt()]).then_inc(cc_kv), sig=False)

                def q_back(j, t_, ctx):
                    hb0 = (j - 1) * 4
                    s2 = (t_ + j) % 2
                    outs = nr_back(ctx, lambda h, s2=s2: qTs[s2][:, h, :], dst_war=stage_free.get(("qTs", s2), []))
                    dst = qT_d.rearrange("p (h t) -> p h t", h=8)[:, hb0:hb0 + 4, t_ * 128:(t_ + 1) * 128]
                    stage_free[("qTs", s2)] = [P.dma("sync", dst, qTs[s2][:], deps=outs)]

                for j in (1, 2):
                    pend = []
                    for t in range(NT):
                        k, ev = proj_tokmajor(j, t)
                        if len(pend) >= DELAY:
                            q_back(j, *pend.pop(0))
                        evs_ps, ctx = nr_front(t, k, ev, 0, 4, wq, wqs)
                        pend.append((t, ctx))
                        ps_free[k] = evs_ps
                        if t == NT - 1:
                            wb_free[j] = ev
                    for t_, ctx in pend:
                        q_back(j, t_, ctx)
                    load_block(j + 2)

                gi = 0
                sg_free = {0: None, 1: None}
                th_free = {0: None, 1: None}
                for j in (3, 4):
                    c0 = (j - 3) * 512
                    for t in range(NT):
                        k, ev = proj_tokmajor(j, t)
                        s_ = gi % 2
                        gi += 1
                        e_th = P.op("scalar", act(th[s_][:], psB[k][:], AF.Tanh, scale=0.5),
                                    deps=[ev, th_free[s_]])
                        e_sg = P.op("vector", stt(sg[s_][:], th[s_][:], 1.0, psB[k][:], ALU.add, ALU.mult),
                                    deps=[e_th, sg_free[s_]])
                        th_free[s_] = e_sg
                        ps_free[k] = [e_sg]
                        sg_free[s_] = P.dma("sync", sga_d[t * 128:(t + 1) * 128, c0:c0 + 512], sg[s_][:], deps=[e_sg])
                        if t == NT - 1:
                            wb_free[j] = ev
                    load_block(j + 2)

                xi = 0
                xst_free = {0: None, 1: None}
                for j in (5, 6):
                    for c in range(4):
                        s_ = xi % 2
                        xi += 1
                        ch0 = (j - 5) * 512 + c * 128
                        evs = []
                        for tb in range(4):
                            k, ev = proj_featmajor(j, c, tb * 512, 512)
                            e_x = P.op("scalar" if tb % 2 else "vector",
                                       (act(xst[s_][:, 2 + tb * 512:2 + (tb + 1) * 512], psB[k][:], AF.Copy) if tb % 2
                                        else cp(xst[s_][:, 2 + tb * 512:2 + (tb + 1) * 512], psB[k][:])),
                                       deps=[ev, xst_free[s_]])
                            ps_free[k] = [e_x]
                            evs.append(e_x)
                        k, ev = proj_featmajor(j, c, T, 128)
                        e_h1 = P.op("vector", cp(xst[s_][:, 0:2], psB[k][:, 0:2]), deps=[ev, xst_free[s_]])
                        e_h2 = P.op("vector", cp(xst[s_][:, T + 2:T + 3], psB[k][:, 2:3]), deps=[ev])
                        ps_free[k] = [e_h1, e_h2]
                        evs += [e_h1, e_h2]
                        xst_free[s_] = P.dma("sync", xrT_d[ch0:ch0 + 128, :], xst[s_][:], deps=evs)
                        if c == 3:
                            wb_free[j] = ev
                    if j + 2 < 9:
                        load_block(j + 2)

                for j in (7, 8):
                    for c in range(4):
                        ch0 = (j - 7) * 512 + c * 128
                        for tb in range(4):
                            k, ev = proj_featmajor(j, c, tb * 512, 512)
                            s_ = gi % 2
                            gi += 1
                            e_th = P.op("scalar", act(th[s_][:], psB[k][:], AF.Tanh, scale=0.5),
                                        deps=[ev, th_free[s_]])
                            e_sg = P.op("vector", stt(sg[s_][:], th[s_][:], 1.0, psB[k][:], ALU.add, ALU.mult),
                                        deps=[e_th, sg_free[s_]])
                            th_free[s_] = e_sg
                            ps_free[k] = [e_sg]
                            sg_free[s_] = P.dma("sync", sgl_d[ch0:ch0 + 128, tb * 512:(tb + 1) * 512], sg[s_][:],
                                                deps=[e_sg])

        with contextlib.ExitStack() as esL:
            consts = esL.enter_context(nc.sbuf_tensor("Lconst", [128, 160], F32))
            cw = consts[:, 0:32]
            cb = consts[:, 32:40]
            hba = consts[:, 40:56]
            hbx = consts[:, 56:72]
            chalf = consts[:, 72:88]
            lnw_s = consts[:, 88:96]
            hown = consts[:, 96:112]
            ctmp = consts[:, 112:128]
            pbias = consts[:, 128:144]
            with phase(nc, "L1") as P:
                wab = P.sb("wab", [128, 2, 8, 128], BF16)
                wxb = P.sb("wxb", [128, 2, 8, 128], BF16)
                XP = [P.sb(f"XP{i}", [128, T + 3], F32) for i in range(2)]
                xc = P.sb("xc", [128, T], F32)
                xfb = P.sb("xfb", [128, T], BF16)
                tha = [P.sb(f"tha{i}", [128, T], F32) for i in range(2)]
                thx = [P.sb(f"thx{i}", [128, T], F32) for i in range(2)]
                av = [P.sb(f"a{i}", [128, T], F32) for i in range(2)]
                a2 = [P.sb(f"a2{i}", [128, T], F32) for i in range(2)]
                hl = [P.sb(f"hl{i}", [128, T], F32) for i in range(2)]
                sth = P.sb("sth", [128, 16], F32)
                crl = P.sb("crl", [128, 2, 8, 2], F32)
                raw = P.sb("raw", [128, 48], F32)
                za = P.ps("za", [128, T], F32)
                zx = P.ps("zx", [128, T], F32)

                e0 = [P.dma("sync", cw, convw), P.dma("sync", cb, convb), P.dma("sync", lnw_s, lnw),
                      P.dma("sync", raw[:, 0:16], lba), P.dma("sync", raw[:, 16:32], lbx),
                      P.dma("sync", raw[:, 32:48], llam)]
                e_wa = P.dma("gpsimd", wab[:], lwa.rearrange("d b c o -> c d b o"))
                e_wx = P.dma("gpsimd", wxb[:], lwx.rearrange("d b c o -> c d b o"))
                c1 = P.op("vector", ts(hba, raw[:, 0:16], 0.5, None, ALU.mult), deps=e0)
                c2 = P.op("vector", ts(hbx, raw[:, 16:32], 0.5, None, ALU.mult), deps=e0)
                c3 = P.op("scalar", act(ctmp, raw[:, 32:48], AF.Exp, scale=-1.0), deps=e0)
                c4 = P.op("scalar", act(ctmp, ctmp, AF.Ln, bias=1.0), deps=[c3])
                c5 = P.op("vector", ts(chalf, ctmp, -4.0, None, ALU.mult), deps=[c4])
                c6 = P.op("vector", ts(pbias, chalf, float(T), None, ALU.mult), deps=[c5])
                cdeps = [c1, c2, c5, c6]

                xp_free = {0: [], 1: []}
                xp_ld = {}

                def load_xp(b):
                    xp_ld[b] = P.dma("sync", XP[b % 2][:], xrT_d[b * 128:(b + 1) * 128, :], deps=xp_free[b % 2])

                load_xp(0)
                prev = {"xc": [], "xfb": [], "za": [], "zx": [], 0: {}, 1: {}}
                spill = {0: {}, 1: {}}
                for b in range(8):
                    if b + 1 < 8:
                        load_xp(b + 1)
                    X = XP[b % 2]
                    e = P.op("vector", ts(xc[:], X[:, 0:T], cw[:, b * 4:b * 4 + 1], cb[:, b:b + 1], ALU.mult, ALU.add),
                             deps=[xp_ld[b]] + e0 + prev["xc"])
                    for jt in range(1, 4):
                        e = P.op("vector", stt(xc[:], X[:, jt:jt + T], cw[:, b * 4 + jt:b * 4 + jt + 1], xc[:],
                                               ALU.mult, ALU.add), deps=[e])
                    e_xc = e
                    xp_free[b % 2] = [e_xc]
                    e_xfb = P.op("gpsimd", cp(xfb[:], xc[:]), deps=[e_xc] + prev["xfb"])
                    ev_a = {}
                    ev_u = {}
                    ev_a2 = {}
                    ev_P = {}
                    for d in range(2):
                        db = d * 8 + b
                        pd = prev[d]
                        la = lx = None
                        for tb in range(4):
                            la = P.op("tensor", mm(za[:, tb * 512:(tb + 1) * 512], wab[:, d, b, :],
                                                   xfb[:, tb * 512:(tb + 1) * 512], True, True),
                                      deps=[e_xfb, e_wa] + prev["za"], sig=(tb == 3))
                        for tb in range(4):
                            lx = P.op("tensor", mm(zx[:, tb * 512:(tb + 1) * 512], wxb[:, d, b, :],
                                                   xfb[:, tb * 512:(tb + 1) * 512], True, True),
                                      deps=[e_xfb, e_wx] + prev["zx"], sig=(tb == 3))
                        e_tha = P.op("scalar", act(tha[d][:], za[:], AF.Tanh, scale=0.5, bias=hba[:, db:db + 1],
                                                   accum_out=sth[:, db:db + 1]), deps=[la] + cdeps + pd.get("tha", []))
                        e_thx = P.op("scalar", act(thx[d][:], zx[:], AF.Tanh, scale=0.5, bias=hbx[:, db:db + 1]),
                                     deps=[lx] + pd.get("thx", []))
                        prev["za"] = [e_tha]
                        prev["zx"] = [e_thx]
                        e_a = P.op("scalar", act(av[d][:], tha[d][:], AF.Exp, scale=chalf[:, db:db + 1],
                                                 bias=chalf[:, db:db + 1]), deps=[e_tha] + pd.get("a", []))
                        e_a2 = P.op("gpsimd", tt(a2[d][:], av[d][:], av[d][:], ALU.mult),
                                    deps=[e_a] + pd.get("a2", []))
                        e_u = P.op("vector", stt(thx[d][:], thx[d][:], 1.0, xc[:], ALU.add, ALU.mult),
                                   deps=[e_thx, e_xc])
                        e_P = P.op("scalar", act(crl[:, d, b, 0:1], sth[:, db:db + 1], AF.Exp,
                                                 scale=chalf[:, db:db + 1], bias=pbias[:, db:db + 1]),
                                   deps=[e_tha])
                        ev_a[d], ev_u[d], ev_a2[d] = e_a, e_u, e_a2
                        ev_P[d] = e_P
                        pd["tha"] = [e_a]
                    prev["xfb"] = [lx]
                    for d in range(2):
                        db = d * 8 + b
                        pd = prev[d]
                        e_m = P.op("scalar", act(a2[d][:], a2[d][:], AF.Sqrt, scale=-1.0, bias=1.0), deps=[ev_a2[d]])
                        e_b = P.op("vector", stt(thx[d][:], thx[d][:], 0.5, a2[d][:], ALU.mult, ALU.mult),
                                   deps=[ev_u[d], e_m])
                        if d == 0:
                            e_sc = P.op("vector", lambda e, d=d: e.tensor_tensor_scan(
                                out=hl[d][:], data0=av[d][:], data1=thx[d][:], initial=0.0,
                                op0=ALU.mult, op1=ALU.add), deps=[e_b, ev_a[d]] + pd.get("hl", []))
                            e_E = P.op("vector", cp(crl[:, d, b, 1:2], hl[d][:, T - 1:T]), deps=[e_sc])
                        else:
                            e_sc = P.op("vector", lambda e, d=d: e.tensor_tensor_scan(
                                out=hl[d][:, ::-1], data0=av[d][:, ::-1], data1=thx[d][:, ::-1], initial=0.0,
                                op0=ALU.mult, op1=ALU.add), deps=[e_b, ev_a[d]] + pd.get("hl", []))
                            e_E = P.op("vector", cp(crl[:, d, b, 1:2], hl[d][:, 0:1]), deps=[e_sc])
                        e_P = ev_P[d]
                        s1 = P.dma("sync", ab_d[db * 2], av[d][:], deps=[ev_a[d]])
                        s2 = P.dma("sync", ab_d[db * 2 + 1], thx[d][:], deps=[e_b])
                        pd["hl"] = [e_E]
                        pd["a"] = [s1, e_sc, ev_a2[d]]
                        pd["thx"] = [s2, e_sc]
                        pd["a2"] = [e_b]
                        spill[d][b] = [e_E, e_P]
                    prev["xc"] = [ev_u[0], ev_u[1], e_xfb]
                alls = [x for d in range(2) for b in range(8) for x in spill[d][b]]
                e_cl = P.dma("sync", cr_loc.ap(), crl[:].rearrange("p d b k -> p (d b k)"), deps=alls)
                P.op("gpsimd", lambda e: e.collective_compute(
                    "AllGather", ALU.bypass, replica_groups=[list(range(NCORES))],
                    ins=[cr_loc.ap().opt()], outs=[cr_all.ap().opt()]).then_inc(cc_cr), deps=[e_cl], sig=False)

            with phase(nc, "L2") as P:
                CA = P.sb("CA", [128, 8, 2, 8, 2], F32)
                r1h = P.sb("r1h", [128, 8], F32)
                Hs = P.sb("Hs", [128, 2, 9, 8], F32)
                hsel = P.sb("hsel", [128, 16], F32)
                ab = [[P.sb(f"ab{i}{k}", [128, T], F32) for k in range(2)] for i in range(4)]
                LO = P.sb("LO", [128, 8, T], F32)
                hb_ = P.sb("hbk", [128, T], F32)
                sq = P.sb("sq", [128, T], F32)
                rsb = P.sb("rsb", [128, T], F32)
                sgl = [ab[2][0], ab[3][0]]
                mxo = [P.sb(f"mxo{i}", [128, T], BF16) for i in range(2)]
                ones = P.sb("ones", [128, 128], F32)
                ssp = P.ps("ssp", [128, T], F32)

                ab_free = {0: [], 1: [], 2: [], 3: []}
                ld = {}

                def load_ab(i):
                    d, b = i % 2, i // 2
                    db = d * 8 + b
                    s_ = i % 4
                    ld[i] = [P.dma("sync", ab[s_][0][:], ab_d[db * 2], deps=ab_free[s_]),
                             P.dma("sync", ab[s_][1][:], ab_d[db * 2 + 1], deps=ab_free[s_])]

                e_1h = P.dma("sync", r1h[:], rank1h)
                for i_ in range(4):
                    load_ab(i_)
                e_w8 = P.op("sync", lambda e: e.nop(), deps=[(cc_cr, 1)])
                e_ca = P.dma("sync", CA[:].rearrange("p r d b k -> p r (d b k)"),
                             cr_all.ap().rearrange("(r p) c -> p r c", p=128), deps=[e_w8])
                e_on = P.op("gpsimd", lambda e: e.memset(ones[:], 1.0))
                e = P.op("vector", lambda e: e.memset(Hs[:], 0.0))
                for r in range(7):
                    e1 = P.op("vector", tt(Hs[:, 0, r + 1, :], CA[:, r, 0, :, 0], Hs[:, 0, r, :], ALU.mult),
                              deps=[e, e_ca])
                    e = P.op("vector", tt(Hs[:, 0, r + 1, :], Hs[:, 0, r + 1, :], CA[:, r, 0, :, 1], ALU.add), deps=[e1])
                for r in range(7, 0, -1):
                    e1 = P.op("vector", tt(Hs[:, 1, r - 1, :], CA[:, r, 1, :, 0], Hs[:, 1, r, :], ALU.mult),
                              deps=[e, e_ca])
                    e = P.op("vector", tt(Hs[:, 1, r - 1, :], Hs[:, 1, r - 1, :], CA[:, r, 1, :, 1], ALU.add), deps=[e1])
                e = P.op("vector", lambda e_: e_.memset(hown, 0.0), deps=[e])
                for d in range(2):
                    for r in range(8):
                        e = P.op("vector", stt(hown[:, d * 8:(d + 1) * 8], Hs[:, d, r, :], r1h[:, r:r + 1],
                                               hown[:, d * 8:(d + 1) * 8], ALU.mult, ALU.add), deps=[e, e_1h])
                e_h = e

                lo_ev = {}
                hb_free = []
                for i in range(16):
                    d, b = i % 2, i // 2
                    db = d * 8 + b
                    s_ = i % 4
                    if d == 0:
                        e_sc = P.op("vector", lambda e, s_=s_, b=b, db=db: e.tensor_tensor_scan(
                            out=LO[:, b, :], data0=ab[s_][0][:], data1=ab[s_][1][:], initial=hown[:, db:db + 1],
                            op0=ALU.mult, op1=ALU.add), deps=ld[i] + [e_h])
                        lo_ev[b] = e_sc
                    else:
                        e_sc = P.op("vector", lambda e, s_=s_, db=db: e.tensor_tensor_scan(
                            out=hb_[:, ::-1], data0=ab[s_][0][:, ::-1], data1=ab[s_][1][:, ::-1],
                            initial=hown[:, db:db + 1], op0=ALU.mult, op1=ALU.add), deps=ld[i] + [e_h] + hb_free)
                        e_ad = P.op("gpsimd", tt(LO[:, b, :], LO[:, b, :], hb_[:], ALU.add), deps=[e_sc, lo_ev[b]])
                        hb_free = [e_ad]
                        lo_ev[b] = e_ad
                    ab_free[s_] = [e_sc]
                    if i + 4 < 16:
                        load_ab(i + 4)
                sq_free = []
                last = None
                for b in range(8):
                    e_sq = P.op("gpsimd", tt(sq[:], LO[:, b, :], LO[:, b, :], ALU.mult), deps=[lo_ev[b]] + sq_free)
                    for tb in range(4):
                        last = P.op("tensor", mm(ssp[:, tb * 512:(tb + 1) * 512], ones[:], sq[:, tb * 512:(tb + 1) * 512],
                                                 b == 0, b == 7), deps=[e_sq, e_on], sig=(tb == 3))
                    sq_free = [last]
                e1 = P.op("vector", ts(rsb[:], ssp[:], 1.0 / 1024, EPS, ALU.mult, ALU.add), deps=[last])
                e2 = P.op("scalar", act(rsb[:], rsb[:], AF.Sqrt), deps=[e1])
                e_rs = P.op("vector", lambda e: e.reciprocal(out=rsb[:], in_=rsb[:]), deps=[e2])
                sg_free = {0: [], 1: []}
                mx_free = {0: [], 1: []}
                for b in range(8):
                    s_ = b % 2
                    e_l = P.dma("sync", sgl[s_][:], sgl_d[b * 128:(b + 1) * 128, :], deps=sg_free[s_] + ab_free[2 + s_])
                    e_n = P.op("vector", stt(LO[:, b, :], LO[:, b, :], lnw_s[:, b:b + 1], rsb[:], ALU.mult, ALU.mult),
                               deps=[e_rs, lo_ev[b], last])
                    e_g = P.op("vector", stt(mxo[s_][:], LO[:, b, :], 0.5, sgl[s_][:], ALU.mult, ALU.mult),
                               deps=[e_n, e_l] + mx_free[s_])
                    sg_free[s_] = [e_g]
                    mx_free[s_] = [P.dma("sync", mixL_d[:, b * T:(b + 1) * T], mxo[s_][:], deps=[e_g])]

        with phase(nc, "C") as P:
            KT = P.sb("KT", [128, 2, S], BF16)
            VA = P.sb("VA", [128, 128, 2, 129], BF16)
            QT = [P.sb(f"QT{i}", [128, 8, 512], BF16) for i in range(2)]
            PT = [P.sb(f"PT{i}", [128, 1536], BF16) for i in range(3)]
            AO = [P.sb(f"AO{i}", [128, 4, 1024], F32) for i in range(2)]
            rden = P.sb("rden", [128, 4 * 64], F32)
            SP = [P.ps(f"SP{i}", [128, 1536], F32) for i in range(2)]
            OA = [P.ps(f"OA{i}", [128, 512], F32) for i in range(2)]

            e_w8 = P.op("sync", lambda e: e.nop(), deps=[(cc_kv, 2)])
            e_kt = []
            e_va = []
            for r in range(NCORES):
                e_kt.append(P.dma("sync", KT[:, :, r * T:(r + 1) * T],
                                  kT_all.ap()[r * 128:(r + 1) * 128, :].rearrange("p (h t) -> p h t", h=2),
                                  deps=[e_w8]))
                e_va.append(P.dma("sync", VA[:, r * NT:(r + 1) * NT, :, :].rearrange("p t h d -> p (t h d)"),
                                  v_all.ap()[r * 128:(r + 1) * 128, :], deps=[e_w8]))
            groups = []
            j0 = 0
            while j0 < 128:
                n = min(3, 128 - j0)
                groups.append((j0, n))
                j0 += n
            NG = len(groups)
            qt_free = {0: [], 1: []}
            qt_ld = {}

            def load_q(qb):
                src = qT_d.rearrange("p (h t) -> p h t", h=8)[:, :, qb * 512:(qb + 1) * 512]
                qt_ld[qb] = P.dma("sync", QT[qb % 2][:], src, deps=qt_free[qb % 2])

            load_q(0)
            sp_free = {0: None, 1: None}
            pt_free = {0: None, 1: None, 2: None}
            oa_free = []
            ao_free = {0: [], 1: []}
            gi = 0
            it = 0
            NQB = int(os.environ.get('MK_QB', '4'))
            for qb in range(NQB):
                if qb + 1 < NQB:
                    load_q(qb + 1)
                Q = QT[qb % 2]
                A = AO[qb % 2]
                ev_h = []
                for h in range(8):
                    kv = h // 4
                    qk_ev = {}
                    ex_ev = {}

                    def issue_qk(g):
                        j0, n = groups[g]
                        k = (gi + g) % 2
                        last = None
                        for jj in range(n):
                            j = j0 + jj
                            QN = int(os.environ.get('MK_QN', '512'))
                            last = P.op("tensor", mm(SP[k][:, jj * 512:jj * 512 + QN], KT[:, kv, j * 128:(j + 1) * 128],
                                                     Q[:, h, 0:QN], True, True),
                                        deps=[e_kt[j // NT], qt_ld[qb], sp_free[k]], sig=(jj == n - 1))
                        qk_ev[g] = last

                    def issue_exp(g):
                        j0, n = groups[g]
                        k = (gi + g) % 2
                        p = (gi + g) % 3
                        ex_ev[g] = P.op("scalar", act(PT[p][:, 0:n * 512], SP[k][:, 0:n * 512], AF.Exp, scale=SCALE),
                                        deps=[qk_ev[g], pt_free[p]])
                        sp_free[k] = ex_ev[g]

                    def issue_pv(g):
                        j0, n = groups[g]
                        p = (gi + g) % 3
                        last = None
                        for jj in range(n):
                            j = j0 + jj
                            for i in range(4):
                                VN = int(os.environ.get('MK_VN', '129'))
                                o = OA[i // 2][:, (i % 2) * 129:(i % 2) * 129 + VN]
                                last = P.op("tensor", mm(o, PT[p][:, jj * 512 + i * 128:jj * 512 + (i + 1) * 128],
                                                         VA[:, j, kv, 0:VN], j == 0 and i % 2 == 0, j == 127, skip=True),
                                            deps=[ex_ev[g], e_va[j // NT]] + (oa_free if j == 0 else []),
                                            sig=(jj == n - 1 and i == 3))
                        pt_free[p] = last
                        return last

                    issue_qk(0)
                    last_pv = None
                    for g in range(NG):
                        if g + 1 < NG:
                            issue_qk(g + 1)
                        issue_exp(g)
                        last_pv = issue_pv(g)
                    gi += NG
                    evs = []
                    for i in range(4):
                        o = OA[i // 2]
                        c0 = (i % 2) * 129
                        col = (it % 64) * 4 + i
                        e_r = P.op("vector", lambda e, o=o, c0=c0, col=col: e.reciprocal(
                            out=rden[:, col:col + 1], in_=o[:, c0 + 128:c0 + 129]), deps=[last_pv])
                        evs.append(P.op("vector", ts(A[:, i, h * 128:(h + 1) * 128], o[:, c0:c0 + 128],
                                                     rden[:, col:col + 1], None, ALU.mult),
                                        deps=[e_r] + ao_free[qb % 2]))
                    oa_free = evs
                    ev_h += evs
                    it += 1
                qt_free[qb % 2] = [last_pv]
                st = P.dma("sync", ao_d[qb * 512:(qb + 1) * 512, :].rearrange("(i p) c -> p i c", p=128), A[:],
                           deps=ev_h)
                ao_free[qb % 2] = [st]

        with phase(nc, "D") as P:
            WO = P.sb("WO", [128, KC, D], BF16)
            mTa = P.sb("mTa", [128, KC, T], BF16)
            anb = P.sb("anb", [128, 1024], F32)
            aot = [P.sb(f"aot{i}", [128, 1024], F32) for i in range(2)]
            sgt = [P.sb(f"sgt{i}", [128, 1024], F32) for i in range(2)]
            xt = [P.sb(f"xt{i}", [128, D], F32) for i in range(2)]
            yt = [P.sb(f"yt{i}", [128, D], F32) for i in range(2)]
            mb = [P.sb(f"mb{i}", [128, 1024], BF16) for i in range(2)]
            stgD = [P.sb(f"stgD{i}", [128, D], F32) for i in range(2)]
            junk = P.sb("junk", [128, 1024], BF16)
            ss = P.sb("ss", [128, NT], F32)
            ms = P.sb("ms", [128, NT], F32)
            sd = P.sb("sd", [128, NT], F32)
            rs = P.sb("rs", [128, NT], F32)
            tpD = [P.ps(f"tp{i}", [128, 1024], BF16) for i in range(2)]
            psD = [P.ps(f"ps{i}", [128, 512], F32) for i in range(4)]

            e_an = P.dma("sync", anb[:], anw.partition_broadcast(128))
            e_ml = P.dma("sync", mTa[:, 8:16, :], mixL_d.rearrange("p (b t) -> p b t", b=8))
            free = {}
            ld = {}

            def load_t(t):
                s_ = t % 2
                ld[t] = dict(
                    ao=P.dma("sync", aot[s_][:], ao_d[t * 128:(t + 1) * 128, :], deps=free.get(("aot", s_), [])),
                    sg=P.dma("sync", sgt[s_][:], sga_d[t * 128:(t + 1) * 128, :], deps=free.get(("sgt", s_), [])),
                )

            e_wo = []
            sfree = {0: None, 1: None}

            def load_wo(kc):
                e_l = P.dma("sync", stgD[kc % 2][:], w_out[kc * 128:(kc + 1) * 128, :], deps=[sfree[kc % 2]])
                e_c = P.op("scalar", act(WO[:, kc, :], stgD[kc % 2][:], AF.Copy), deps=[e_l])
                sfree[kc % 2] = e_c
                e_wo.append(e_c)

            load_t(0)
            tp_free = {0: [], 1: []}
            pre = {}
            for t in range(NT):
                if t + 1 < NT:
                    load_t(t + 1)
                load_wo(t)
                s_ = t % 2
                L = ld[t]
                e_sq = P.op("scalar", act(junk[:], aot[s_][:], AF.Square, accum_out=ss[:, t:t + 1]), deps=[L["ao"]])
                e_r = rstd_chain(P, ss[:, t:t + 1], ms[:, t:t + 1], sd[:, t:t + 1], rs[:, t:t + 1], 1.0 / 1024, e_sq)
                e_n = P.op("vector", stt(aot[s_][:], aot[s_][:], rs[:, t:t + 1], anb[:], ALU.mult, ALU.mult),
                           deps=[e_r, e_an, e_sq])
                e_g = P.op("vector", stt(mb[s_][:], aot[s_][:], 0.5, sgt[s_][:], ALU.mult, ALU.mult),
                           deps=[e_n, L["sg"]] + free.get(("mb", s_), []))
                free[("aot", s_)] = [e_g]
                free[("sgt", s_)] = [e_g]
                last = None
                for h in range(8):
                    last = P.op("tensor", tr(tpD[s_][:, h * 128:(h + 1) * 128], mb[s_][:, h * 128:(h + 1) * 128], ident[:]),
                                deps=[e_g] + tp_free[s_], sig=(h == 7))
                free[("mb", s_)] = [last]
                e_cp = P.op("scalar", act(mTa[:, 0:8, t * 128:(t + 1) * 128],
                                          tpD[s_][:].rearrange("p (k n) -> p k n", k=8), AF.Copy), deps=[last])
                tp_free[s_] = [e_cp]
                pre[t] = e_cp

            xl = {}

            def load_x(t):
                xl[t] = P.dma("sync", xt[t % 2][:], xe[t * 128:(t + 1) * 128, :], deps=free.get(("xt", t % 2), []))

            load_x(0)
            ps_free = {k: [] for k in range(4)}
            for t in range(NT):
                if t + 1 < NT:
                    load_x(t + 1)
                s_ = t % 2
                outs = []
                lastmm = None
                for cbk in range(4):
                    for kc in range(KC):
                        lastmm = P.op("tensor", mm(psD[cbk][:], mTa[:, kc, t * 128:(t + 1) * 128],
                                                   WO[:, kc, cbk * 512:(cbk + 1) * 512], kc == 0, kc == KC - 1),
                                      deps=[pre[t], e_ml, e_wo[kc]] + (ps_free[cbk] if kc == 0 else []),
                                      sig=(kc == KC - 1))
                    e_y = P.op("vector", tt(yt[s_][:, cbk * 512:(cbk + 1) * 512], psD[cbk][:],
                                            xt[s_][:, cbk * 512:(cbk + 1) * 512], ALU.add),
                               deps=[lastmm, xl[t]] + free.get(("yt", s_), []))
                    ps_free[cbk] = [e_y]
                    outs.append(e_y)
                free[("xt", s_)] = outs
                free[("yt", s_)] = [P.dma("sync", out[t * 128:(t + 1) * 128, :], yt[s_][:], deps=outs)]
        clear_all()
    return nc


_NC_CACHE = {}


def _rope_tables():
    pos = np.arange(S)
    row = (pos // 64).astype(np.float32)
    col = (pos % 64).astype(np.float32)
    inv = (np.float32(10000.0) ** (-np.arange(32, dtype=np.float32) / np.float32(32))).astype(np.float32)
    ar = (row[:, None] * inv[None, :]).astype(np.float32)
    ac = (col[:, None] * inv[None, :]).astype(np.float32)
    cr, sr, cc, sc = np.cos(ar), np.sin(ar), np.cos(ac), np.sin(ac)
    C = np.concatenate([cr, cr, cc, cc], axis=1).astype(np.float32)
    Sg = np.concatenate([-sr, sr, -sc, sc], axis=1).astype(np.float32)
    return C, Sg


def kernel(x, norm_w, w_in, q_norm_w, k_norm_w, conv_w, conv_b, lru_wa, lru_ba, lru_wx, lru_bx, lru_lambda,
           attn_norm_w, lru_norm_w, w_out):
    f = lambda a: np.ascontiguousarray(np.asarray(a, dtype=np.float32))
    x = f(x).reshape(S, D)
    if "nc" not in _NC_CACHE:
        _NC_CACHE["nc"] = build_program()
    nc = _NC_CACHE["nc"]
    C, Sg = _rope_tables()
    ident = np.eye(128, dtype=np.float32).astype(ml_dtypes.bfloat16)
    pm = lambda v: np.ascontiguousarray(f(v).reshape(-1, 128).T)
    shared = {
        "w_in": f(w_in), "w_out": f(w_out), "norm_w": f(norm_w), "q_norm_w": f(q_norm_w), "k_norm_w": f(k_norm_w),
        "attn_norm_w": f(attn_norm_w), "ident": ident,
        "convw": np.ascontiguousarray(f(conv_w).reshape(4, 8, 128).transpose(2, 1, 0).reshape(128, 32)),
        "convb": pm(conv_b), "lba": pm(lru_ba), "lbx": pm(lru_bx), "llam": pm(lru_lambda), "lnw": pm(lru_norm_w),
        "lru_wa": f(lru_wa), "lru_wx": f(lru_wx),
    }
    in_maps = []
    for r in range(NCORES):
        xe = np.zeros((TE, D), np.float32)
        xe[:T] = x[r * T:(r + 1) * T]
        if r > 0:
            xe[T:T + 2] = x[r * T - 2:r * T]
        if r < NCORES - 1:
            xe[T + 2] = x[(r + 1) * T]
        oh = np.zeros((128, 8), np.float32)
        oh[:, r] = 1.0
        m = dict(shared)
        m.update({"xe": xe, "ropeC": np.ascontiguousarray(C[r * T:(r + 1) * T]),
                  "ropeS": np.ascontiguousarray(Sg[r * T:(r + 1) * T]), "rank1h": oh})
        in_maps.append(m)
    if os.environ.get("MK_TRACE"):
        res = run_bass_kernel_spmd(nc, in_maps, core_ids=list(range(NCORES)), trace=True)
        print("EXEC_TIME_NS", res.exec_time_ns)
    else:
        res = run_bass_kernel_spmd(nc, in_maps, core_ids=list(range(NCORES)))
    _NC_CACHE["last"] = res
    outp = np.concatenate([np.asarray(res.results[r]["out"], dtype=np.float32) for r in range(NCORES)], axis=0)
    return outp.reshape(1, S, D)
```
